# Optimizing a Trainium2 kernel written in Bass

```python
import math
import jax, jax.numpy as jnp
from jax import lax
import numpy as np

D_MODEL = 2048
BATCH = 4
SEQ = 4096
DEPTH = 4

N_MEM = 256
A_HEADS = 8
A_DQK = 64
A_DV = 128
A_WIDTH = A_HEADS * A_DV
B_HEADS = 8
B_DK = 128
B_DV = 128
B_WIDTH = B_HEADS * B_DV
HGRN_CHUNK = 64
C_HEADS = 4
C_DH = 256
C_WIDTH = C_HEADS * C_DH
N_BRANCH = 3
BRANCH_WIDTH = 1024
Q_BLOCK = 128
EPS = 1e-6
IN_SIZES = (
    2 * A_HEADS * A_DQK,
    2 * A_HEADS * A_DQK,
    A_WIDTH,
    A_WIDTH,
    B_HEADS * B_DK,
    B_WIDTH,
    B_HEADS * B_DK,
    B_WIDTH,
    C_WIDTH,
    C_WIDTH,
    N_BRANCH * D_MODEL,
)
IN_WIDTH = 16384

kernel_name = "hybrid_diffattn_hgrn2_memxattn_gated_merge"


def rms_norm(x, g):
    xf = x.astype(jnp.float32)
    y = xf * lax.rsqrt(jnp.mean(xf * xf, axis=-1, keepdims=True) + EPS)
    return (y * g.astype(jnp.float32)).astype(x.dtype)


def split_columns(p):
    parts, off = [], 0
    for n in IN_SIZES:
        parts.append(p[..., off:off + n])
        off += n
    return parts


def diff_attention(qa, ka, va, lam_p, subln_g, lam_init):
    B, S, _ = qa.shape
    q = qa.reshape(B, S, A_HEADS, 2, A_DQK).transpose(0, 2, 3, 1, 4) * (A_DQK ** -0.5)
    k = ka.reshape(B, S, A_HEADS, 2, A_DQK).transpose(0, 2, 3, 1, 4)
    v = va.reshape(B, S, A_HEADS, A_DV).transpose(0, 2, 1, 3)
    lp = lam_p.astype(jnp.float32)
    lam = jnp.exp(jnp.sum(lp[0] * lp[1])) - jnp.exp(jnp.sum(lp[2] * lp[3])) + lam_init
    outs = []
    for i in range(S // Q_BLOCK):
        lo, hi = i * Q_BLOCK, (i + 1) * Q_BLOCK
        s = jnp.einsum('bhmqd,bhmkd->bhmqk', q[:, :, :, lo:hi].astype(jnp.float32),
                       k[:, :, :, :hi].astype(jnp.float32))
        mask = jnp.arange(hi)[None, :] <= (lo + jnp.arange(Q_BLOCK))[:, None]
        pr = jax.nn.softmax(jnp.where(mask, s, -jnp.inf), axis=-1)
        attn = pr[:, :, 0] - lam * pr[:, :, 1]
        outs.append(jnp.einsum('bhqk,bhkd->bhqd', attn.astype(v.dtype), v[:, :, :hi]))
    o = jnp.concatenate(outs, axis=2)
    o = rms_norm(o, subln_g) * (1.0 - lam_init)
    return o.transpose(0, 2, 1, 3).reshape(B, S, A_WIDTH)


def hgrn2_chunked(q, k, v, log_f):
    B, S, H, dk = q.shape
    dv = v.shape[-1]
    nc = S // HGRN_CHUNK

    def to_chunks(t):
        return t.reshape(B, nc, HGRN_CHUNK, H, t.shape[-1]).transpose(1, 0, 3, 2, 4)

    causal = jnp.tril(jnp.ones((HGRN_CHUNK, HGRN_CHUNK), dtype=bool))

    def step(state, inp):
        qc, kc, vc, gc = inp
        b = jnp.cumsum(gc, axis=2)
        o_inter = jnp.einsum('bhtd,bhde->bhte', qc * jnp.exp(b), state)
        diff = b[:, :, :, None, :] - b[:, :, None, :, :]
        decay = jnp.exp(jnp.where(causal[:, :, None], diff, -jnp.inf))
        a = jnp.einsum('bhtd,bhsd,bhtsd->bhts', qc, kc, decay)
        o = o_inter + jnp.einsum('bhts,bhse->bhte', a, vc)
        b_last = b[:, :, -1:, :]
        k_dec = kc * jnp.exp(b_last - b)
        state = jnp.exp(b_last[:, :, 0, :])[..., None] * state + jnp.einsum('bhsd,bhse->bhde', k_dec, vc)
        return state, o

    s0 = jnp.zeros((B, H, dk, dv), jnp.float32)
    _, o = lax.scan(step, s0, (to_chunks(q), to_chunks(k), to_chunks(v), to_chunks(log_f)))
    return o.transpose(1, 0, 3, 2, 4).reshape(B, S, H, dv)


def hgrn2_branch(fb, ib, qb, lb, hgrn_g):
    B, S, _ = fb.shape
    lbl = lb.reshape(B_HEADS, B_DK)
    z = fb.reshape(B, S, B_HEADS, B_DK).astype(jnp.float32)
    log_f = jnp.logaddexp(jnp.log(lbl), jnp.log1p(-lbl) + jax.nn.log_sigmoid(z))
    k = (1.0 - lbl) * jax.nn.sigmoid(-z)
    q = jax.nn.silu(qb.reshape(B, S, B_HEADS, B_DK).astype(jnp.float32))
    v = ib.reshape(B, S, B_HEADS, B_DV).astype(jnp.float32)
    o = hgrn2_chunked(q, k, v, log_f)
    o = rms_norm(o, hgrn_g)
    return o.reshape(B, S, B_WIDTH).astype(fb.dtype)


def memory_cross_attention(qc, mem_n, w_kv):
    B, S, _ = qc.shape
    kv = mem_n @ w_kv
    km = kv[..., :C_WIDTH].reshape(B, N_MEM, C_HEADS, C_DH)
    vm = kv[..., C_WIDTH:].reshape(B, N_MEM, C_HEADS, C_DH)
    qh = qc.reshape(B, S, C_HEADS, C_DH)
    s = jnp.einsum('bqhd,bkhd->bhqk', qh.astype(jnp.float32), km.astype(jnp.float32)) * (C_DH ** -0.5)
    p = jax.nn.softmax(s, axis=-1).astype(vm.dtype)
    return jnp.einsum('bhqk,bkhd->bqhd', p, vm).reshape(B, S, C_WIDTH)


def setup_inputs(seed: int = 0) -> dict:
    key = jax.random.key(seed)
    ks = jax.random.split(key, 14)
    f32 = jnp.float32
    return {
        "x": jax.random.normal(ks[0], (BATCH, SEQ, D_MODEL), f32),
        "mem": jax.random.normal(ks[1], (BATCH, N_MEM, D_MODEL), f32),
        "norm_g": 1.0 + 0.02 * jax.random.normal(ks[2], (DEPTH, D_MODEL), f32),
        "w_in": jax.random.normal(ks[3], (DEPTH, D_MODEL, IN_WIDTH), f32) * D_MODEL ** -0.5,
        "diff_lambda": 0.1 * jax.random.normal(ks[4], (DEPTH, 4, A_DQK), f32),
        "diff_subln_g": 1.0 + 0.02 * jax.random.normal(ks[5], (DEPTH, A_DV), f32),
        "hgrn_lb_raw": 0.1 * jax.random.normal(ks[6], (DEPTH, B_HEADS * B_DK), f32),
        "hgrn_norm_g": 1.0 + 0.02 * jax.random.normal(ks[7], (DEPTH, B_DV), f32),
        "mem_norm_g": 1.0 + 0.02 * jax.random.normal(ks[8], (DEPTH, D_MODEL), f32),
        "w_kv_mem": jax.random.normal(ks[9], (DEPTH, D_MODEL, 2 * C_WIDTH), f32) * D_MODEL ** -0.5,
        "w_branch": jax.random.normal(ks[10], (DEPTH, N_BRANCH, BRANCH_WIDTH, D_MODEL), f32) * BRANCH_WIDTH ** -0.5,
        "w_out": jax.random.normal(ks[11], (DEPTH, D_MODEL, D_MODEL), f32) * D_MODEL ** -0.5,
        "final_norm_g": 1.0 + 0.02 * jax.random.normal(ks[12], (D_MODEL,), f32),
    }


def reference(x, mem, norm_g, w_in, diff_lambda, diff_subln_g, hgrn_lb_raw, hgrn_norm_g,
              mem_norm_g, w_kv_mem, w_branch, w_out, final_norm_g):
    lb_all = jnp.cumsum(jax.nn.softmax(hgrn_lb_raw.astype(jnp.float32), axis=0), axis=0)
    lb_all = lb_all - lb_all[0:1]
    for l in range(DEPTH):
        lam_init = 0.8 - 0.6 * math.exp(-0.3 * l)
        h = rms_norm(x, norm_g[l])
        p = h @ w_in[l]
        qa, ka, va, ga, fb, ib, qb, gb, qc, gc, gl = split_columns(p)
        oa = diff_attention(qa, ka, va, diff_lambda[l], diff_subln_g[l], lam_init) * jax.nn.silu(ga)
        ob = hgrn2_branch(fb, ib, qb, lb_all[l], hgrn_norm_g[l]) * jax.nn.silu(gb)
        mem_n = rms_norm(mem, mem_norm_g[l])
        oc = memory_cross_attention(qc, mem_n, w_kv_mem[l]) * jax.nn.silu(gc)
        gates = jax.nn.sigmoid(gl.reshape(gl.shape[0], gl.shape[1], N_BRANCH, D_MODEL))
        y = (gates[:, :, 0] * (oa @ w_branch[l, 0])
             + gates[:, :, 1] * (ob @ w_branch[l, 1])
             + gates[:, :, 2] * (oc @ w_branch[l, 2]))
        x = x + y @ w_out[l]
    return rms_norm(x, final_norm_g)
```

```python
import numpy as np
import math
import contextlib
import concourse.bass as bass
import concourse.mybir as mybir
from concourse.bass_utils import run_bass_kernel_spmd

F32 = mybir.dt.float32
BF16 = mybir.dt.bfloat16
ALU = mybir.AluOpType
AF = mybir.ActivationFunctionType

D = 2048
SEQ = 4096
DEPTH = 4
DC = 16
NMEM = 256
EPS = 1e-6
INW = 16384
OFF = dict(qa=0, ka=1024, va=2048, ga=3072, fb=4096, ib=5120, qb=6144, gb=7168,
           qc=8192, gc=9216, gl=10240)
NEG = -30000.0


class _Proxy:
    def __getattr__(self, name):
        def f(*a, **k):
            return (name, a, k)
        return f


PROXY = _Proxy()


class Tok:
    __slots__ = ("sem", "val", "dma")

    def __init__(self, sem, val, dma):
        self.sem, self.val, self.dma = sem, val, dma


class Buf:
    def __init__(self, t, name=""):
        self.t = t
        self.name = name
        self.w = {}
        self.r = {}

    def __getitem__(self, idx):
        return self.t[idx]


class Ctx:
    def __init__(self, nc):
        self.nc = nc
        self.prog = {"pe": [], "act": [], "dve": [], "pool": [], "sp": []}
        self.meta = {"pe": [], "act": [], "dve": [], "pool": [], "sp": []}
        self.dma_issued = {}
        self.sems = {}
        self.engs = {}
        self.stack = None

    def sem(self, name):
        s = self.stack.enter_context(self.nc.semaphore(name))
        return s


class Eng:
    def __init__(self, ctx, name, sem):
        self.ctx, self.name, self.sem = ctx, name, sem
        self.cnt = 0
        self.seen = {}
        self.pending = []

    def _need(self, tok):
        if tok.dma:
            val = self.ctx.dma_issued[id(tok.sem)]
        else:
            val = tok.val
        key = id(tok.sem)
        if self.seen.get(key, 0) >= val:
            return
        self.seen[key] = val
        self.pending.append((tok.sem, val))

    def deps(self, reads=(), writes=()):
        for b in reads:
            for t in b.w.values():
                self._need(t)
        for b in writes:
            for t in b.w.values():
                self._need(t)
            for t in b.r.values():
                self._need(t)

    def _flush(self):
        p = self.pending
        self.pending = []
        return p

    def op(self, fn, reads=(), writes=(), inc=True):
        self.deps(reads, writes)
        waits = self._flush()
        if inc:
            self.cnt += 1
            tok = Tok(self.sem, self.cnt, False)
        else:
            tok = None
        sem = self.sem
        call = fn(PROXY)

        def rec(e, waits=waits, call=call, inc=inc):
            for (s, v) in waits:
                e.wait_ge(s, v)
            ins = getattr(e, call[0])(*call[1], **call[2])
            if inc:
                ins.then_inc(sem, 1)

        self.ctx.prog[self.name].append(rec)
        self.ctx.meta[self.name].append((waits, [(sem, 1)] if inc else []))
        if tok is not None:
            for b in reads:
                b.r[id(sem)] = tok
            for b in writes:
                b.w[id(sem)] = tok
        return tok

    def dma(self, out_ap, in_ap, dsem, reads=(), writes=(), cc=None):
        self.deps(reads, writes)
        waits = self._flush()
        ctx = self.ctx
        if cc is None:
            ctx.dma_issued[id(dsem)] = ctx.dma_issued.get(id(dsem), 0) + 16
        else:
            ctx.dma_issued[id(dsem)] = ctx.dma_issued.get(id(dsem), 0) + 1
        tok = Tok(dsem, ctx.dma_issued[id(dsem)], True)
        call = cc(PROXY) if cc is not None else None

        def rec(e, waits=waits, call=call):
            for (s, v) in waits:
                e.wait_ge(s, v)
            if call is None:
                e.dma_start(out=out_ap, in_=in_ap).then_inc(dsem, 16)
            else:
                getattr(e, call[0])(*call[1], **call[2]).then_inc(dsem)

        ctx.prog[self.name].append(rec)
        ctx.meta[self.name].append((waits, [(dsem, 16 if cc is None else 1)]))
        for b in reads:
            b.r[id(dsem)] = tok
        for b in writes:
            b.w[id(dsem)] = tok
        return tok

    def wait_all(self, bufs):
        self.deps(reads=bufs, writes=bufs)
        waits = self._flush()
        if waits:
            def rec(e, waits=waits):
                for (s, v) in waits:
                    e.wait_ge(s, v)
            self.ctx.prog[self.name].append(rec)
            self.ctx.meta[self.name].append((waits, []))


class Ring:
    def __init__(self, bufs):
        self.bufs = bufs
        self.i = 0

    def next(self):
        b = self.bufs[self.i % len(self.bufs)]
        self.i += 1
        return b


W_IN_SZ = 256 * INW
W_KV_SZ = 256 * 2048
W_BR_SZ = 3 * 128 * 2048
W_OUT_SZ = 256 * 2048
WSH = W_IN_SZ + W_KV_SZ + W_BR_SZ + W_OUT_SZ
WCOLS = WSH // 128
RG = [[0, 1], [2, 3], [4, 5], [6, 7]]
RG8 = [list(range(8))]
PHASES = ("p1", "p2", "p3", "p4", "p5")


def build(n_layers=DEPTH, T=2048, dbg=False, stop_after=None):
    nc = bass.Bass("TRN2", target_bir_lowering=False)
    ctx = Ctx(nc)
    NQT = T // 512
    NKB = T // 128
    NCH = T // 64

    def din(name, shape, dt=F32):
        return nc.dram_tensor(name, list(shape), dt, kind="ExternalInput").ap()

    xT_in = din("xT", [D, T])
    memT_in = din("memT", [D, NMEM])
    wsh_in = din("wsh", [n_layers, 128, WCOLS])
    normg_in = din("normg", [128, DEPTH * DC])
    memg_in = din("memg", [128, DEPTH * DC])
    fing_in = din("fing", [128, DC])
    subg_in = din("subg", [128, DEPTH])
    hgg_in = din("hgg", [128, DEPTH])
    lamb_in = din("lamb", [128, DEPTH * 4 * 64])
    lbraw_in = din("lbraw", [128, DEPTH * 8])
    ident_in = din("ident", [128, 128])
    masks_in = din("masks", [128, 4 * 512])
    flags_in = din("flags", [128, 2])
    outT = nc.dram_tensor("outT", [D, T], F32, kind="ExternalOutput").ap()

    DBG_OUT = ("oaT", "obT", "ocT")

    def dscr(name, shape, dt):
        if dbg and name in DBG_OUT:
            return nc.dram_tensor(name, list(shape), dt, kind="ExternalOutput")
        return nc.dram_tensor(name, list(shape), dt)

    xs_t = dscr("xs", [D, T], F32)
    qaT_t = dscr("qaT", [1024, T], BF16)
    NPC = max(1, T // 512)
    KR = 1024 // NPC
    VR = T // NPC
    kaTl_t = [dscr(f"kaTl{i}", [KR, T], BF16) for i in range(NPC)]
    kaTg_t = [dscr(f"kaTg{i}", [2 * KR, T], BF16) for i in range(NPC)]
    val_t = [dscr(f"val{i}", [VR, 1024], BF16) for i in range(NPC)]
    vag_t = [dscr(f"vag{i}", [2 * VR, 1024], BF16) for i in range(NPC)]
    gaT_t = dscr("gaT", [1024, T], BF16)
    sgT_t = dscr("sgT", [1024, T], F32)
    ib_t = dscr("ibs", [T, 1024], BF16)
    qbT_t = dscr("qbT", [1024, T], BF16)
    gbT_t = dscr("gbT", [1024, T], BF16)
    qcT_t = dscr("qcT", [1024, T], BF16)
    gcT_t = dscr("gcT", [1024, T], BF16)
    glT_t = dscr("glT", [6144, T], BF16)
    oaT_t = dscr("oaT", [1024, T], BF16)
    obT_t = dscr("obT", [1024, T], BF16)
    ocT_t = dscr("ocT", [1024, T], BF16)
    obl_t = dscr("obl", [1024, T], F32)
    qgT_t = dscr("qgT", [1024, T], BF16)
    sfl_t = dscr("sfl", [1024, 128], BF16)
    sfg_t = dscr("sfg", [2048, 128], BF16)
    wbn_t = [dscr(f"wbn{l}", [128, WCOLS], F32) for l in range(n_layers)]
    wf_t = [dscr(f"wf{l}", [1024, WCOLS], F32) for l in range(n_layers)]

    with contextlib.ExitStack() as top:
        ctx.stack = top
        PE = Eng(ctx, "pe", ctx.sem("s_pe"))
        ACT = Eng(ctx, "act", ctx.sem("s_act"))
        DVE = Eng(ctx, "dve", ctx.sem("s_dve"))
        POOL = Eng(ctx, "pool", ctx.sem("s_pool"))
        SP = Eng(ctx, "sp", ctx.sem("s_sp"))
        engs = [PE, ACT, DVE, POOL, SP]
        all_dsems = []
        uniq = [0]

        sem_pool = []
        phase_sems = [None]

        def dsem(name):
            if phase_sems[0] is not None and sem_pool:
                s = sem_pool.pop()
                phase_sems[0].append(s)
                return s
            uniq[0] += 1
            s = ctx.sem(f"{name}_{uniq[0]}")
            ctx.dma_issued[id(s)] = 0
            all_dsems.append(s)
            if phase_sems[0] is not None:
                phase_sems[0].append(s)
            return s

        def phase_begin():
            phase_sems[0] = []

        def phase_end():
            sem_pool.extend(phase_sems[0])
            phase_sems[0] = None

        def sb(stack, name, shape, dt):
            uniq[0] += 1
            return Buf(stack.enter_context(nc.sbuf_tensor(f"sb{uniq[0]}_{name}", list(shape), dt)), name)

        def ps(stack, name, shape, dt=F32):
            uniq[0] += 1
            return Buf(stack.enter_context(nc.psum_tensor(f"ps{uniq[0]}_{name}", list(shape), dt)), name)

        def barrier():
            toks = [(e.sem, e.cnt) for e in engs if e.cnt > 0]
            toks += [(s, ctx.dma_issued[id(s)]) for s in all_dsems if ctx.dma_issued[id(s)] > 0]
            for e in engs:
                waits = []
                for (s, v) in toks:
                    if e.seen.get(id(s), 0) < v:
                        e.seen[id(s)] = v
                        waits.append((s, v))
                if waits:
                    def rec(en, waits=waits):
                        for (s, v) in waits:
                            en.wait_ge(s, v)
                    ctx.prog[e.name].append(rec)
                    ctx.meta[e.name].append((waits, []))

        def rsem(n, name):
            return [dsem(f"{name}{i}") for i in range(n)]

        class LRing:
            def __init__(self, st, name, n, shape, dt):
                self.bufs = [sb(st, f"{name}{i}", shape, dt) for i in range(n)]
                self.sems = rsem(n, "d_" + name)
                self.i = 0

            def next(self):
                k = self.i % len(self.bufs)
                self.i += 1
                return self.bufs[k], self.sems[k]

        xs = Buf(xs_t, "xs")
        qaT, gaT = Buf(qaT_t), Buf(gaT_t)
        kaTl, kaTg, val, vag = (Buf(None) for _ in range(4))
        sgT, ibs, qbT, gbT, qcT, gcT, glT = (Buf(t) for t in (sgT_t, ib_t, qbT_t, gbT_t, qcT_t, gcT_t, glT_t))
        oaT, obT, ocT, obl, qgT, sfl, sfg = (Buf(t) for t in (oaT_t, obT_t, ocT_t, obl_t, qgT_t, sfl_t, sfg_t))
        wbn = [Buf(t) for t in wbn_t]
        wf = [Buf(t) for t in wf_t]

        cc_chain = Buf(None, "cc_chain")

        def cc_allgather(src_t, dst_t, srcb, dstb, sem, groups):
            POOL.dma(None, None, sem, reads=[srcb], writes=[dstb, cc_chain],
                     cc=lambda e: e.collective_compute("AllGather", ALU.bypass, replica_groups=groups,
                                                       ins=[src_t.ap().opt()], outs=[dst_t.ap().opt()]))

        s_wb = dsem("d_wbn")
        s_wg = [dsem(f"d_wag{l}") for l in range(n_layers)]
        def bounce_weights(l):
            for i in range(4):
                SP.dma(wbn_t[l][i * 32:(i + 1) * 32, :], wsh_in[l, i * 32:(i + 1) * 32, :], s_wb, writes=[wbn[l]])

        def gather_weights(l):
            cc_allgather(wbn_t[l], wf_t[l], wbn[l], wf[l], s_wg[l], RG8)

        bounce_weights(0)
        gather_weights(0)

        def wviews(l):
            flat = wf_t[l].ap().rearrange("(r a) b -> r (a b)", r=8)
            o = 0
            w_in_v = flat[:, o:o + W_IN_SZ].rearrange("r (i n) -> r i n", n=INW); o += W_IN_SZ
            w_kv_v = flat[:, o:o + W_KV_SZ].rearrange("r (i n) -> r i n", n=2048); o += W_KV_SZ
            w_br_v = flat[:, o:o + W_BR_SZ].rearrange("r (b i n) -> r b i n", b=3, n=2048); o += W_BR_SZ
            w_out_v = flat[:, o:o + W_OUT_SZ].rearrange("r (i n) -> r i n", n=2048)
            return w_in_v, w_kv_v, w_br_v, w_out_v

        normg = sb(top, "normg", [128, DEPTH * DC], F32)
        memg = sb(top, "memg", [128, DEPTH * DC], F32)
        fing = sb(top, "fing", [128, DC], F32)
        subg = sb(top, "subg", [128, DEPTH], F32)
        hgg = sb(top, "hgg", [128, DEPTH], F32)
        flags = sb(top, "flags", [128, 2], F32)
        ident = sb(top, "identb", [128, 128], BF16)
        ones = sb(top, "onesb", [128, 128], BF16)
        masks = sb(top, "masksb", [128, 4 * 512], BF16)
        lamv = sb(top, "lamv", [128, 2 * DEPTH], F32)
        lbv = sb(top, "lbv", [128, DEPTH * 8], F32)
        omlb = sb(top, "omlb", [128, DEPTH * 8], F32)
        zcol = sb(top, "zcol", [128, 1], F32)
        epsc = sb(top, "epsc", [128, 1], F32)
        kmT = sb(top, "kmT", [128, 8, NMEM], BF16)
        vm = sb(top, "vm", [128, 2, 1024], BF16)
        s_const = dsem("d_const")
        s_constp = dsem("d_constp")
        lam_init = [0.8 - 0.6 * math.exp(-0.3 * l) for l in range(DEPTH)]

        with contextlib.ExitStack() as st:
            lamb = sb(st, "lamb", [128, DEPTH * 256], F32)
            lamp = sb(st, "lamp", [128, DEPTH * 128], F32)
            lams = sb(st, "lams", [128, DEPTH * 2], F32)
            lbraw = sb(st, "lbraw", [128, DEPTH * 8], F32)
            lbe = sb(st, "lbe", [128, DEPTH * 8], F32)
            lbs = sb(st, "lbs", [128, 8], F32)
            for (dst, src) in ((normg, normg_in), (memg, memg_in), (fing, fing_in), (subg, subg_in),
                               (hgg, hgg_in), (flags, flags_in), (lamb, lamb_in), (lbraw, lbraw_in)):
                SP.dma(dst[:], src, s_const, writes=[dst])
            POOL.dma(ident[:], ident_in, s_constp, writes=[ident])
            POOL.dma(masks[:], masks_in, s_constp, writes=[masks])
            DVE.op(lambda e: e.memset(ones[:], 1.0), writes=[ones])
            DVE.op(lambda e: e.memset(zcol[:], 0.0), writes=[zcol])
            DVE.op(lambda e: e.memset(epsc[:], EPS), writes=[epsc])
            lb3 = lamb.t[:].rearrange("p (l f d) -> p l f d", l=DEPTH, f=4)
            lp3 = lamp.t[:].rearrange("p (l f d) -> p l f d", l=DEPTH, f=2)
            for l in range(DEPTH):
                for j in range(2):
                    DVE.op(lambda e, l=l, j=j: e.tensor_tensor(out=lp3[:, l, j, :], in0=lb3[:, l, 2 * j, :],
                                                               in1=lb3[:, l, 2 * j + 1, :], op=ALU.mult),
                           reads=[lamb], writes=[lamp])
            DVE.op(lambda e: e.tensor_reduce(out=lams[:], in_=lamp.t[:].rearrange("p (a d) -> p a d", d=64),
                                             axis=mybir.AxisListType.X, op=ALU.add),
                   reads=[lamp], writes=[lams])
            ACT.op(lambda e: e.activation(out=lams[:], in_=lams[:], func=AF.Exp), reads=[lams], writes=[lams])
            for l in range(DEPTH):
                DVE.op(lambda e, l=l: e.scalar_tensor_tensor(out=lamv[:, l:l + 1], in0=lams[:, 2 * l + 1:2 * l + 2],
                                                             scalar=-lam_init[l], in1=lams[:, 2 * l:2 * l + 1],
                                                             op0=ALU.add, op1=ALU.subtract),
                       reads=[lams], writes=[lamv])
                DVE.op(lambda e, l=l: e.tensor_scalar(out=lamv[:, 4 + l:5 + l], in0=subg[:, l:l + 1],
                                                      scalar1=1.0 - lam_init[l], scalar2=None, op0=ALU.mult),
                       reads=[subg], writes=[lamv])
            ACT.op(lambda e: e.activation(out=lbe[:], in_=lbraw[:], func=AF.Exp), reads=[lbraw], writes=[lbe])
            DVE.op(lambda e: e.tensor_tensor(out=lbs[:], in0=lbe[:, 0:8], in1=lbe[:, 8:16], op=ALU.add),
                   reads=[lbe], writes=[lbs])
            DVE.op(lambda e: e.tensor_tensor(out=lbs[:], in0=lbs[:], in1=lbe[:, 16:24], op=ALU.add),
                   reads=[lbe, lbs], writes=[lbs])
            DVE.op(lambda e: e.tensor_tensor(out=lbs[:], in0=lbs[:], in1=lbe[:, 24:32], op=ALU.add),
                   reads=[lbe, lbs], writes=[lbs])
            DVE.op(lambda e: e.reciprocal(out=lbs[:], in_=lbs[:]), reads=[lbs], writes=[lbs])
            for l in range(DEPTH):
                DVE.op(lambda e, l=l: e.tensor_tensor(out=lbe[:, 8 * l:8 * l + 8], in0=lbe[:, 8 * l:8 * l + 8],
                                                      in1=lbs[:], op=ALU.mult),
                       reads=[lbe, lbs], writes=[lbe])
            DVE.op(lambda e: e.memset(lbv[:, 0:8], 0.0), writes=[lbv])
            for l in range(1, DEPTH):
                DVE.op(lambda e, l=l: e.tensor_tensor(out=lbv[:, 8 * l:8 * l + 8], in0=lbv[:, 8 * l - 8:8 * l],
                                                      in1=lbe[:, 8 * l:8 * l + 8], op=ALU.add),
                       reads=[lbe, lbv], writes=[lbv])
            DVE.op(lambda e: e.tensor_scalar(out=omlb[:], in0=lbv[:], scalar1=-1.0, scalar2=1.0,
                                             op0=ALU.mult, op1=ALU.add), reads=[lbv], writes=[omlb])
            s_x0 = dsem("d_x0")
            for i in range(4):
                SP.dma(xs_t[i * 512:(i + 1) * 512, :], xT_in[i * 512:(i + 1) * 512, :], s_x0, writes=[xs])
            barrier()

        def rstd_from(out_ap, in_ap, scale, outb, inb):
            ACT.op(lambda e: e.activation(out=out_ap, in_=in_ap, func=AF.Ln, bias=epsc[:, 0:1], scale=scale),
                   reads=[inb, epsc], writes=[outb])
            ACT.op(lambda e: e.activation(out=out_ap, in_=out_ap, func=AF.Exp, scale=-0.5),
                   reads=[outb], writes=[outb])

        def rms_tiles(st, src_ap_fn, g_col_fn, n_tok, tile, dst_fn, tag, src_bufs, after):
            xin = LRing(st, f"xin{tag}", 2, [128, DC, tile], F32)
            sq = Ring([sb(st, f"sq{tag}{i}", [128, DC, tile], BF16) for i in range(2)])
            rstd = Ring([sb(st, f"rstd{tag}{i}", [128, tile], F32) for i in range(2)])
            pss = Ring([ps(st, f"pss{tag}{i}", [128, tile]) for i in range(2)])
            for it in range(n_tok // tile):
                t0 = it * tile
                xb, xsem = xin.next()
                SP.dma(xb[:], src_ap_fn(t0, tile), xsem, reads=src_bufs, writes=[xb])
                sqb = sq.next()
                for dc in range(DC):
                    ACT.op(lambda e, xb=xb, sqb=sqb, dc=dc: e.activation(out=sqb[:, dc, :], in_=xb[:, dc, :],
                                                                         func=AF.Square),
                           reads=[xb], writes=[sqb])
                pb = pss.next()
                for dc in range(DC):
                    PE.op(lambda e, dc=dc, pb=pb, sqb=sqb: e.matmul(out=pb[:], lhsT=ones[:], rhs=sqb[:, dc, :],
                                                                  start=(dc == 0), stop=(dc == DC - 1)),
                          reads=[ones, sqb], writes=[pb], inc=(dc == DC - 1))
                rb = rstd.next()
                rstd_from(rb[:], pb[:], 1.0 / D, rb, pb)
                for dc in range(DC):
                    oap, obuf = dst_fn(dc, t0, tile)
                    DVE.op(lambda e, dc=dc, oap=oap, xb=xb, rb=rb: e.scalar_tensor_tensor(
                        out=oap, in0=xb[:, dc, :], scalar=g_col_fn(dc), in1=rb[:], op0=ALU.mult, op1=ALU.mult),
                        reads=[xb, rb], writes=[obuf])
                if after is not None:
                    after(it, t0, tile)

        xs_v = xs_t.ap().rearrange("(c p) t -> p c t", p=128)
        memT_v = memT_in.rearrange("(c p) t -> p c t", p=128)
        outT_v = outT.rearrange("(c p) t -> p c t", p=128)
        s_cc = [dsem("d_cc0"), dsem("d_cc1"), dsem("d_cc2")]
        stop = False

        for l in range(n_layers):
            w_in_v, w_kv_v, w_br_v, w_out_v = wviews(l)
            phase_begin()
            with contextlib.ExitStack() as st:
                hT = sb(st, "hT", [128, DC, T], BF16)
                mnT = sb(st, "mnT", [128, DC, NMEM], BF16)
                with contextlib.ExitStack() as st2:
                    rms_tiles(st2, lambda t0, n: xs_v[:, :, t0:t0 + n],
                              lambda dc: normg[:, l * DC + dc:l * DC + dc + 1],
                              T, 256, lambda dc, t0, n: (hT[:, dc, t0:t0 + n], hT), "a", [xs], None)
                    rms_tiles(st2, lambda t0, n: memT_v[:, :, t0:t0 + n],
                              lambda dc: memg[:, l * DC + dc:l * DC + dc + 1],
                              NMEM, 256, lambda dc, t0, n: (mnT[:, dc, t0:t0 + n], mnT), "m", [], None)
                    barrier()
                wring = LRing(st, "wblk", 3, [128, DC, 512], BF16)
                pring = Ring([ps(st, f"pp{i}", [128, 512]) for i in range(6)])
                stg_b = LRing(st, "stgb", 3, [128, T], BF16)
                stg_f = LRing(st, "stgf", 2, [128, T], F32)
                stg_t = LRing(st, "stgt", 4, [128, 512], BF16)

                def wload16(view, c0):
                    wb, wsem = wring.next()
                    for r in range(8):
                        POOL.dma(wb[:, 2 * r:2 * r + 2, :],
                                 view[r, :, c0:c0 + 512].rearrange("(h p) n -> p h n", p=128),
                                 wsem, reads=[wf[l]], writes=[wb])
                    return wb

                def evac(kind, out_ap, in_ap, rb, wb_):
                    if kind == "copy":
                        DVE.op(lambda e: e.tensor_copy(out=out_ap, in_=in_ap), reads=[rb], writes=[wb_])
                    elif kind == "silu":
                        ACT.op(lambda e: e.activation(out=out_ap, in_=in_ap, func=AF.Silu), reads=[rb], writes=[wb_])
                    elif kind == "sigm":
                        ACT.op(lambda e: e.activation(out=out_ap, in_=in_ap, func=AF.Sigmoid), reads=[rb], writes=[wb_])
                    else:
                        ACT.op(lambda e: e.activation(out=out_ap, in_=in_ap, func=AF.Copy, scale=float(kind)),
                               reads=[rb], writes=[wb_])

                def fm_block(c0, kind, dstb, row0, f32=False):
                    wb = wload16(w_in_v, c0)
                    for j in range(4):
                        sg, ssem = (stg_f if f32 else stg_b).next()
                        for tt in range(NQT):
                            pb = pring.next()
                            for dc in range(DC):
                                PE.op(lambda e, dc=dc, pb=pb, wb=wb, j=j, tt=tt: e.matmul(
                                    out=pb[:], lhsT=wb[:, dc, j * 128:(j + 1) * 128],
                                    rhs=hT[:, dc, tt * 512:(tt + 1) * 512], start=(dc == 0), stop=(dc == DC - 1)),
                                    reads=[wb, hT], writes=[pb], inc=(dc == DC - 1))
                            evac(kind, sg[:, tt * 512:(tt + 1) * 512], pb[:], pb, sg)
                        r0 = row0 + j * 128
                        if dstb is kaTl:
                            dst_ap = kaTl_t[r0 // KR][r0 % KR:r0 % KR + 128, :]
                        else:
                            dst_ap = dstb.t[r0:r0 + 128, :]
                        SP.dma(dst_ap, sg[:], ssem, reads=[sg], writes=[dstb])

                def tm_block(c0, dstb, col0, m):
                    wb = wload16(w_in_v, c0)
                    for tb in range(T // m):
                        pb = pring.next()
                        for dc in range(DC):
                            PE.op(lambda e, dc=dc, pb=pb, wb=wb, tb=tb: e.matmul(
                                out=pb[0:m, :], lhsT=hT[:, dc, tb * m:(tb + 1) * m], rhs=wb[:, dc, :],
                                start=(dc == 0), stop=(dc == DC - 1)),
                                reads=[wb, hT], writes=[pb], inc=(dc == DC - 1))
                        sg, ssem = stg_t.next()
                        evac("copy", sg[0:m, :], pb[0:m, :], pb, sg)
                        r0 = tb * m
                        if dstb is val:
                            dst_ap = val_t[r0 // VR][r0 % VR:r0 % VR + m, col0:col0 + 512]
                        else:
                            dst_ap = dstb.t[r0:r0 + m, col0:col0 + 512]
                        SP.dma(dst_ap, sg[0:m, :], ssem, reads=[sg], writes=[dstb])

                for i in range(2):
                    fm_block(OFF["ka"] + i * 512, "copy", kaTl, i * 512)
                for i in range(2):
                    tm_block(OFF["va"] + i * 512, val, i * 512, 128)
                for i in range(NPC):
                    cc_allgather(kaTl_t[i], kaTg_t[i], kaTl, kaTg, s_cc[0], RG)
                for i in range(NPC):
                    cc_allgather(val_t[i], vag_t[i], val, vag, s_cc[1], RG)
                for i in range(2):
                    fm_block(OFF["qa"] + i * 512, 0.125, qaT, i * 512)
                for i in range(2):
                    fm_block(OFF["ga"] + i * 512, "silu", gaT, i * 512)
                for i in range(2):
                    fm_block(OFF["fb"] + i * 512, "sigm", sgT, i * 512, f32=True)
                for i in range(2):
                    tm_block(OFF["ib"] + i * 512, ibs, i * 512, 64)
                for i in range(2):
                    fm_block(OFF["qb"] + i * 512, "silu", qbT, i * 512)
                for i in range(2):
                    fm_block(OFF["gb"] + i * 512, "silu", gbT, i * 512)
                for i in range(2):
                    fm_block(OFF["qc"] + i * 512, 1.0 / 16.0, qcT, i * 512)
                for i in range(2):
                    fm_block(OFF["gc"] + i * 512, "silu", gcT, i * 512)
                for i in range(12):
                    fm_block(OFF["gl"] + i * 512, "sigm", glT, i * 512)
                for cb in range(4):
                    wb = wload16(w_kv_v, cb * 512)
                    if cb < 2:
                        for j in range(4):
                            pb = pring.next()
                            for dc in range(DC):
                                PE.op(lambda e, dc=dc, pb=pb, wb=wb, j=j: e.matmul(
                                    out=pb[:, 0:NMEM], lhsT=wb[:, dc, j * 128:(j + 1) * 128], rhs=mnT[:, dc, :],
                                    start=(dc == 0), stop=(dc == DC - 1)),
                                    reads=[wb, mnT], writes=[pb], inc=(dc == DC - 1))
                            evac("copy", kmT[:, cb * 4 + j, :], pb[:, 0:NMEM], pb, kmT)
                    else:
                        for mb in range(2):
                            pb = pring.next()
                            for dc in range(DC):
                                PE.op(lambda e, dc=dc, pb=pb, wb=wb, mb=mb: e.matmul(
                                    out=pb[:], lhsT=mnT[:, dc, mb * 128:(mb + 1) * 128], rhs=wb[:, dc, :],
                                    start=(dc == 0), stop=(dc == DC - 1)),
                                    reads=[wb, mnT], writes=[pb], inc=(dc == DC - 1))
                            evac("copy", vm[:, mb, (cb - 2) * 512:(cb - 1) * 512], pb[:], pb, vm)
                barrier()
            phase_end()
            if stop_after == "p1":
                break

            if l + 1 < n_layers:
                bounce_weights(l + 1)
                gather_weights(l + 1)
            phase_begin()
            with contextlib.ExitStack() as st:
                KT = LRing(st, "KT", 2, [128, 2 * T], BF16)
                QT = LRing(st, "QT", 2, [128, T], BF16)
                GA = LRing(st, "GA", 2, [128, T], BF16)
                VV = LRing(st, "VV", 2, [128, 2 * NKB, 128], BF16)
                ptr = Ring([sb(st, f"pt{i}", [128, 512], BF16) for i in range(8)])
                recr = Ring([sb(st, f"rec{i}", [128, 512], F32) for i in range(2)])
                omr = Ring([sb(st, f"om{i}", [128, 512], F32) for i in range(4)])
                osr = Ring([sb(st, f"osq{i}", [128, 512], BF16) for i in range(2)])
                rsr = Ring([sb(st, f"rsa{i}", [128, 512], F32) for i in range(2)])
                fin = LRing(st, "fina", 2, [128, 512], BF16)
                accO = [ps(st, f"accO{m}", [128, 512]) for m in range(2)]
                spr = Ring([ps(st, f"sps{i}", [128, 512]) for i in range(6)])
                laccr = [Ring([sb(st, f"lacc{m}{i}", [128, 512], F32) for i in range(2)]) for m in range(2)]
                lbr = Ring([sb(st, f"lb{i}", [128, 512], BF16) for i in range(2)])
                ACCE = [POOL, DVE]
                for h in range(8):
                    ktb, ksem = KT.next()
                    hr = h * 128
                    SP.dma(ktb[:, 0:T], kaTg_t[hr // KR][hr % KR:hr % KR + 128, :], ksem, reads=[kaTg], writes=[ktb])
                    SP.dma(ktb[:, T:2 * T], kaTl_t[hr // KR][hr % KR:hr % KR + 128, :], ksem, reads=[kaTl], writes=[ktb])
                    vb, vsem = VV.next()
                    kbp = VR // 128
                    for i in range(NPC):
                        SP.dma(vb[:, i * kbp:(i + 1) * kbp, :],
                               vag_t[i][0:VR, h * 128:(h + 1) * 128].rearrange("(kb p) d -> p kb d", p=128),
                               vsem, reads=[vag], writes=[vb])
                        SP.dma(vb[:, NKB + i * kbp:NKB + (i + 1) * kbp, :],
                               val_t[i][:, h * 128:(h + 1) * 128].rearrange("(kb p) d -> p kb d", p=128),
                               vsem, reads=[val], writes=[vb])
                    qb_, qsem = QT.next()
                    SP.dma(qb_[:], qaT_t[h * 128:(h + 1) * 128, :], qsem, reads=[qaT], writes=[qb_])
                    gb_, gsem = GA.next()
                    SP.dma(gb_[:], gaT_t[h * 128:(h + 1) * 128, :], gsem, reads=[gaT], writes=[gb_])
                    for qt in range(NQT):
                        blocks = [(kb, True, None) for kb in range(NKB)]
                        for kb in range(4 * qt + 4):
                            blocks.append((NKB + kb, False, (kb - 4 * qt) if kb >= 4 * qt else None))
                        nb = len(blocks)
                        units = [(bi, kbg, prev, dj, m) for bi, (kbg, prev, dj) in enumerate(blocks) for m in range(2)]
                        LOOK = 3
                        pts = {}
                        lacc = [laccr[0].next(), laccr[1].next()]
                        for u in range(len(units) + LOOK):
                            if u < len(units):
                                bi, kbg, prev, dj, m = units[u]
                                sp_ = spr.next()
                                PE.op(lambda e, sp_=sp_, m=m, kbg=kbg: e.matmul(
                                    out=sp_[:], lhsT=ktb[m * 64:(m + 1) * 64, kbg * 128:(kbg + 1) * 128],
                                    rhs=qb_[m * 64:(m + 1) * 64, qt * 512:(qt + 1) * 512], start=True, stop=True),
                                    reads=[ktb, qb_], writes=[sp_])
                                pt = ptr.next()
                                if prev:
                                    ACT.op(lambda e, pt=pt, sp_=sp_: e.activation(out=pt[:], in_=sp_[:], func=AF.Exp,
                                                                                   bias=flags[:, 0:1]),
                                           reads=[sp_, flags], writes=[pt])
                                else:
                                    ACT.op(lambda e, pt=pt, sp_=sp_: e.activation(out=pt[:], in_=sp_[:], func=AF.Exp),
                                           reads=[sp_], writes=[pt])
                                if dj is not None:
                                    DVE.op(lambda e, pt=pt, dj=dj: e.tensor_tensor(
                                        out=pt[:], in0=pt[:], in1=masks[:, dj * 512:(dj + 1) * 512], op=ALU.mult),
                                        reads=[pt, masks], writes=[pt])
                                pts[u] = pt
                            if u >= LOOK:
                                bi, kbg, prev, dj, m = units[u - LOOK]
                                pt = pts.pop(u - LOOK)
                                PE.op(lambda e, m=m, kbg=kbg, pt=pt, bi=bi: e.matmul(
                                    out=accO[m][:], lhsT=vb[:, kbg, :], rhs=pt[:], start=(bi == 0), stop=(bi == nb - 1)),
                                    reads=[vb, pt], writes=[accO[m]])
                                la = lacc[m]
                                if bi == 0:
                                    ACCE[m].op(lambda e, la=la, pt=pt: e.tensor_copy(out=la[:], in_=pt[:]),
                                               reads=[pt], writes=[la])
                                else:
                                    ACCE[m].op(lambda e, la=la, pt=pt: e.tensor_tensor(out=la[:], in0=la[:], in1=pt[:],
                                                                                       op=ALU.add),
                                               reads=[pt, la], writes=[la])
                        oms = []
                        for m in range(2):
                            lb_ = lbr.next()
                            ACCE[m].op(lambda e, lb_=lb_, m=m: e.tensor_copy(out=lb_[:], in_=lacc[m][:]),
                                       reads=[lacc[m]], writes=[lb_])
                            pl = spr.next()
                            PE.op(lambda e, pl=pl, lb_=lb_: e.matmul(out=pl[:], lhsT=ones[:], rhs=lb_[:], start=True, stop=True),
                                  reads=[ones, lb_], writes=[pl])
                            rc = recr.next()
                            DVE.op(lambda e, rc=rc, pl=pl: e.reciprocal(out=rc[:], in_=pl[:]),
                                   reads=[pl], writes=[rc])
                            om = omr.next()
                            DVE.op(lambda e, rc=rc, m=m, om=om: e.tensor_tensor(out=om[:], in0=accO[m][:], in1=rc[:],
                                                                                 op=ALU.mult),
                                   reads=[accO[m], rc], writes=[om])
                            oms.append(om)
                        o = omr.next()
                        DVE.op(lambda e, o=o, oms=oms: e.scalar_tensor_tensor(
                            out=o[:], in0=oms[1][:], scalar=lamv[:, l:l + 1], in1=oms[0][:], op0=ALU.mult, op1=ALU.add),
                            reads=[oms[0], oms[1], lamv], writes=[o])
                        osq = osr.next()
                        ACT.op(lambda e, o=o, osq=osq: e.activation(out=osq[:], in_=o[:], func=AF.Square),
                               reads=[o], writes=[osq])
                        pss = spr.next()
                        PE.op(lambda e, osq=osq, pss=pss: e.matmul(out=pss[:], lhsT=ones[:], rhs=osq[:], start=True, stop=True),
                              reads=[ones, osq], writes=[pss])
                        rs = rsr.next()
                        rstd_from(rs[:], pss[:], 1.0 / 128.0, rs, pss)
                        DVE.op(lambda e, o=o, rs=rs: e.tensor_tensor(out=o[:], in0=o[:], in1=rs[:], op=ALU.mult),
                               reads=[o, rs], writes=[o])
                        fb_, fsem = fin.next()
                        DVE.op(lambda e, o=o, fb_=fb_, gb_=gb_, qt=qt: e.scalar_tensor_tensor(
                            out=fb_[:], in0=o[:], scalar=lamv[:, 4 + l:5 + l], in1=gb_[:, qt * 512:(qt + 1) * 512],
                            op0=ALU.mult, op1=ALU.mult), reads=[o, gb_, lamv], writes=[fb_])
                        SP.dma(oaT_t[h * 128:(h + 1) * 128, qt * 512:(qt + 1) * 512], fb_[:], fsem,
                               reads=[fb_], writes=[oaT])
                barrier()
            phase_end()
            if stop_after == "p2":
                break

            phase_begin()
            with contextlib.ExitStack() as st:
                SG = LRing(st, "SG", 2, [128, T], F32)
                QQ = LRing(st, "QQ", 2, [128, T], BF16)
                VI = LRing(st, "VI", 2, [64, NCH, 128], BF16)
                onesf = sb(st, "onesf", [128, T], F32)
                ff = sb(st, "ff", [128, T], F32)
                kk = sb(st, "kk", [128, T], F32)
                Bx = sb(st, "Bx", [128, T + 1], F32)
                nB = sb(st, "nB", [128, T + 1], F32)
                eB = sb(st, "eB", [128, T], F32)
                qg = LRing(st, "qg", 2, [128, T], BF16)
                oloc = LRing(st, "oloc", 2, [128, T], F32)
                S = sb(st, "Sst", [128, 128], F32)
                Sb = LRing(st, "Sb", 2, [128, 128], BF16)
                Dt = sb(st, "Dt", [128, T], F32)
                Et = sb(st, "Et", [128, T], F32)
                qtl = sb(st, "qtl", [128, T], BF16)
                ktl = sb(st, "ktl", [128, T], BF16)
                khl = sb(st, "khl", [128, T], BF16)
                scl = sb(st, "scl", [128, 2 * NCH], F32)
                amr = Ring([sb(st, f"am{i}", [64, 64], BF16) for i in range(3)])
                ksr = Ring([sb(st, f"ks{i}", [64, 128], BF16) for i in range(3)])
                str_ = Ring([sb(st, f"stb{i}", [128, 128], BF16) for i in range(3)])
                psA = Ring([ps(st, f"psA{i}", [128, 512]) for i in range(2)])
                psT = Ring([ps(st, f"psT{i}", [128, 1024], BF16) for i in range(2)])
                psO = Ring([ps(st, f"psO{i}", [128, 512]) for i in range(2)])
                psS = Ring([ps(st, f"psS{i}", [128, 512]) for i in range(2)])
                DVE.op(lambda e: e.memset(onesf[:], 1.0), writes=[onesf])
                for amb in amr.bufs:
                    DVE.op(lambda e, amb=amb: e.memset(amb[:], 0.0), writes=[amb])
                for h in range(8):
                    col = l * 8 + h
                    sgb, ssem = SG.next()
                    SP.dma(sgb[:], sgT_t[h * 128:(h + 1) * 128, :], ssem, reads=[sgT], writes=[sgb])
                    qqb, qsem = QQ.next()
                    SP.dma(qqb[:], qbT_t[h * 128:(h + 1) * 128, :], qsem, reads=[qbT], writes=[qqb])
                    vib, vsem = VI.next()
                    SP.dma(vib[:], ib_t[:, h * 128:(h + 1) * 128].rearrange("(c p) d -> p c d", p=64), vsem,
                           reads=[ibs], writes=[vib])
                    DVE.op(lambda e, sgb=sgb, col=col: e.tensor_scalar(
                        out=ff[:], in0=sgb[:], scalar1=omlb[:, col:col + 1], scalar2=lbv[:, col:col + 1],
                        op0=ALU.mult, op1=ALU.add), reads=[sgb, omlb, lbv], writes=[ff])
                    DVE.op(lambda e: e.tensor_scalar(out=kk[:], in0=ff[:], scalar1=-1.0, scalar2=1.0,
                                                     op0=ALU.mult, op1=ALU.add), reads=[ff], writes=[kk])
                    ACT.op(lambda e: e.activation(out=ff[:], in_=ff[:], func=AF.Ln), reads=[ff], writes=[ff])
                    DVE.op(lambda e: e.memset(Bx[:, 0:1], 0.0), writes=[Bx])
                    DVE.op(lambda e: e.tensor_tensor_scan(out=Bx[:, 1:T + 1], data0=onesf[:], data1=ff[:], initial=0.0,
                                                          op0=ALU.mult, op1=ALU.add),
                           reads=[onesf, ff], writes=[Bx])
                    DVE.op(lambda e: e.tensor_scalar(out=nB[:], in0=Bx[:], scalar1=-1.0, scalar2=None, op0=ALU.mult),
                           reads=[Bx], writes=[nB])
                    ACT.op(lambda e: e.activation(out=eB[:], in_=Bx[:, 1:T + 1], func=AF.Exp), reads=[Bx], writes=[eB])
                    qgb, qgsem = qg.next()
                    DVE.op(lambda e, qgb=qgb, qqb=qqb: e.tensor_tensor(out=qgb[:], in0=qqb[:], in1=eB[:], op=ALU.mult),
                           reads=[qqb, eB], writes=[qgb])
                    SP.dma(qgT_t[h * 128:(h + 1) * 128, :], qgb[:], qgsem, reads=[qgb], writes=[qgT])
                    Bc = Bx[:, 1:T + 1].rearrange("p (c t) -> p c t", t=64)
                    nBc = nB[:, 1:T + 1].rearrange("p (c t) -> p c t", t=64)
                    rmid = Bx[:, 32:T:64].rearrange("p (c o) -> p c o", o=1).broadcast_to([128, NCH, 64])
                    rend = Bx[:, 64:T + 1:64].rearrange("p (c o) -> p c o", o=1).broadcast_to([128, NCH, 64])
                    D3v = Dt[:].rearrange("p (c t) -> p c t", t=64)
                    E3v = Et[:].rearrange("p (c t) -> p c t", t=64)
                    DVE.op(lambda e: e.tensor_tensor(out=D3v, in0=Bc, in1=rmid, op=ALU.subtract), reads=[Bx], writes=[Dt])
                    ACT.op(lambda e: e.activation(out=Et[:], in_=Dt[:], func=AF.Exp), reads=[Dt], writes=[Et])
                    DVE.op(lambda e: e.tensor_tensor(out=qtl[:], in0=qqb[:], in1=Et[:], op=ALU.mult),
                           reads=[qqb, Et], writes=[qtl])
                    ACT.op(lambda e: e.activation(out=Et[:], in_=Dt[:], func=AF.Exp, scale=-1.0), reads=[Dt], writes=[Et])
                    DVE.op(lambda e: e.tensor_tensor(out=ktl[:], in0=kk[:], in1=Et[:], op=ALU.mult),
                           reads=[kk, Et], writes=[ktl])
                    DVE.op(lambda e: e.tensor_tensor(out=D3v, in0=rend, in1=Bc, op=ALU.subtract), reads=[Bx], writes=[Dt])
                    ACT.op(lambda e: e.activation(out=Et[:], in_=Dt[:], func=AF.Exp), reads=[Dt], writes=[Et])
                    DVE.op(lambda e: e.tensor_tensor(out=khl[:], in0=kk[:], in1=Et[:], op=ALU.mult),
                           reads=[kk, Et], writes=[khl])
                    DVE.op(lambda e: e.tensor_tensor(out=scl[:, 0:NCH], in0=Bx[:, 32:T:64], in1=Bx[:, 0:T:64],
                                                     op=ALU.subtract), reads=[Bx], writes=[scl])
                    DVE.op(lambda e: e.tensor_tensor(out=scl[:, NCH:2 * NCH], in0=Bx[:, 64:T + 1:64], in1=Bx[:, 0:T:64],
                                                     op=ALU.subtract), reads=[Bx], writes=[scl])
                    ACT.op(lambda e: e.activation(out=scl[:], in_=scl[:], func=AF.Exp), reads=[scl], writes=[scl])
                    DVE.op(lambda e: e.memset(S[:], 0.0), writes=[S])
                    olb, olsem = oloc.next()
                    for c in range(NCH if stop_after != 'p3a' else 0):
                        t0 = c * 64
                        ts = slice(t0, t0 + 64)
                        qt_, kt_, kh_ = qtl.t[:, ts], ktl.t[:, ts], khl.t[:, ts]
                        pa = psA.next()
                        PE.op(lambda e, pa=pa, kt_=kt_, qt_=qt_: e.matmul(out=pa[0:32, 0:32], lhsT=kt_[:, 0:32],
                                                                        rhs=qt_[:, 0:32], start=True, stop=True),
                              reads=[ktl, qtl], writes=[pa], inc=False)
                        PE.op(lambda e, pa=pa, kt_=kt_, qt_=qt_: e.matmul(out=pa[0:64, 32:64], lhsT=kt_,
                                                                        rhs=qt_[:, 32:64], start=True, stop=True),
                              reads=[ktl, qtl], writes=[pa])
                        am = amr.next()
                        DVE.op(lambda e, am=am, pa=pa: e.tensor_tensor(out=am[0:32, 0:32], in0=pa[0:32, 0:32],
                                                                       in1=masks[0:32, 0:32], op=ALU.mult),
                               reads=[pa, masks], writes=[am])
                        DVE.op(lambda e, am=am, pa=pa: e.tensor_tensor(out=am[0:64, 32:64], in0=pa[0:64, 32:64],
                                                                       in1=masks[0:64, 32:64], op=ALU.mult),
                               reads=[pa, masks], writes=[am])
                        pt_ = psT.next()
                        PE.op(lambda e, pt_=pt_, kh_=kh_: e.transpose(out=pt_[0:64, 0:128], in_=kh_, identity=ident[:]),
                              reads=[khl, ident], writes=[pt_])
                        ks = ksr.next()
                        ACT.op(lambda e, ks=ks, pt_=pt_: e.activation(out=ks[:], in_=pt_[0:64, 0:128], func=AF.Copy),
                               reads=[pt_], writes=[ks])
                        stb = str_.next()
                        DVE.op(lambda e, stb=stb, c=c: e.tensor_scalar(out=stb[:], in0=S[:], scalar1=scl[:, c:c + 1],
                                                                         scalar2=None, op0=ALU.mult),
                               reads=[S, scl], writes=[stb])
                        po = psO.next()
                        PE.op(lambda e, po=po, vib=vib, c=c, am=am: e.matmul(out=po[:, 0:64], lhsT=vib[:, c, :], rhs=am[:],
                                                                           start=True, stop=False),
                              reads=[vib, am], writes=[po], inc=False)
                        PE.op(lambda e, po=po, stb=stb, qt_=qt_: e.matmul(out=po[:, 0:64], lhsT=stb[:], rhs=qt_,
                                                                        start=False, stop=True),
                              reads=[stb, qtl, vib, am], writes=[po])
                        ACT.op(lambda e, po=po, olb=olb, ts=ts: e.activation(out=olb[:, ts], in_=po[:, 0:64], func=AF.Copy),
                               reads=[po], writes=[olb])
                        pst = psS.next()
                        PE.op(lambda e, pst=pst, ks=ks, vib=vib, c=c: e.matmul(out=pst[:, 0:128], lhsT=ks[:], rhs=vib[:, c, :],
                                                                             start=True, stop=True),
                              reads=[ks, vib], writes=[pst])
                        DVE.op(lambda e, pst=pst, c=c: e.scalar_tensor_tensor(
                            out=S[:], in0=S[:], scalar=scl[:, NCH + c:NCH + c + 1], in1=pst[:, 0:128], op0=ALU.mult, op1=ALU.add),
                            reads=[S, scl, pst], writes=[S])
                    SP.dma(obl_t[h * 128:(h + 1) * 128, :], olb[:], olsem, reads=[olb], writes=[obl])
                    sbb, sbsem = Sb.next()
                    DVE.op(lambda e, sbb=sbb: e.tensor_copy(out=sbb[:], in_=S[:]), reads=[S], writes=[sbb])
                    SP.dma(sfl_t[h * 128:(h + 1) * 128, :], sbb[:], sbsem, reads=[sbb], writes=[sfl])
                if stop_after not in ('p3a', 'p3c'):
                    cc_allgather(sfl_t, sfg_t, sfl, sfg, s_cc[2], RG)
                barrier()
            phase_end()
            if stop_after in ('p3a', 'p3c'):
                break
            phase_begin()
            with contextlib.ExitStack() as st:
                SI = LRing(st, "SI", 2, [128, 128], BF16)
                OL = LRing(st, "OL", 2, [128, T], F32)
                QG = LRing(st, "QG", 2, [128, T], BF16)
                GB = LRing(st, "GB", 2, [128, T], BF16)
                svr = Ring([sb(st, f"sv{i}", [128, 128], BF16) for i in range(2)])
                o3r = Ring([sb(st, f"o3{i}", [128, 512], F32) for i in range(3)])
                osr = Ring([sb(st, f"osqb{i}", [128, 512], BF16) for i in range(2)])
                rsr = Ring([sb(st, f"rsb{i}", [128, 512], F32) for i in range(2)])
                fin = LRing(st, "finb", 2, [128, 512], BF16)
                pcr = Ring([ps(st, f"pcor{i}", [128, 512]) for i in range(2)])
                pssr = Ring([ps(st, f"pssb{i}", [128, 512]) for i in range(2)])
                for h in range(8):
                    sib, sisem = SI.next()
                    SP.dma(sib[:], sfg_t[h * 128:(h + 1) * 128, :], sisem, reads=[sfg], writes=[sib])
                    olb, olsem = OL.next()
                    SP.dma(olb[:], obl_t[h * 128:(h + 1) * 128, :], olsem, reads=[obl], writes=[olb])
                    qgb, qgsem = QG.next()
                    SP.dma(qgb[:], qgT_t[h * 128:(h + 1) * 128, :], qgsem, reads=[qgT], writes=[qgb])
                    gbb, gbsem = GB.next()
                    SP.dma(gbb[:], gbT_t[h * 128:(h + 1) * 128, :], gbsem, reads=[gbT], writes=[gbb])
                    sv = svr.next()
                    DVE.op(lambda e, sv=sv, sib=sib: e.tensor_scalar(out=sv[:], in0=sib[:], scalar1=flags[:, 1:2],
                                                                     scalar2=None, op0=ALU.mult),
                           reads=[sib, flags], writes=[sv])
                    for tt in range(NQT):
                        tsl = slice(tt * 512, (tt + 1) * 512)
                        pc = pcr.next()
                        PE.op(lambda e, pc=pc, sv=sv, qgb=qgb, tsl=tsl: e.matmul(out=pc[:], lhsT=sv[:], rhs=qgb[:, tsl],
                                                                               start=True, stop=True),
                              reads=[sv, qgb], writes=[pc])
                        o = o3r.next()
                        DVE.op(lambda e, o=o, pc=pc, olb=olb, tsl=tsl: e.tensor_tensor(out=o[:], in0=pc[:], in1=olb[:, tsl],
                                                                                      op=ALU.add),
                               reads=[pc, olb], writes=[o])
                        osq = osr.next()
                        ACT.op(lambda e, o=o, osq=osq: e.activation(out=osq[:], in_=o[:], func=AF.Square),
                               reads=[o], writes=[osq])
                        pss_ = pssr.next()
                        PE.op(lambda e, osq=osq, pss_=pss_: e.matmul(out=pss_[:], lhsT=ones[:], rhs=osq[:], start=True, stop=True),
                              reads=[ones, osq], writes=[pss_])
                        rs = rsr.next()
                        rstd_from(rs[:], pss_[:], 1.0 / 128.0, rs, pss_)
                        DVE.op(lambda e, o=o, rs=rs: e.tensor_tensor(out=o[:], in0=o[:], in1=rs[:], op=ALU.mult),
                               reads=[o, rs], writes=[o])
                        fb_, fsem = fin.next()
                        DVE.op(lambda e, o=o, fb_=fb_, gbb=gbb, tsl=tsl: e.scalar_tensor_tensor(
                            out=fb_[:], in0=o[:], scalar=hgg[:, l:l + 1], in1=gbb[:, tsl],
                            op0=ALU.mult, op1=ALU.mult), reads=[o, gbb, hgg], writes=[fb_])
                        SP.dma(obT_t[h * 128:(h + 1) * 128, tsl], fb_[:], fsem, reads=[fb_], writes=[obT])
                barrier()
            phase_end()
            if stop_after == "p3":
                break

            phase_begin()
            with contextlib.ExitStack() as st:
                QC = LRing(st, "QC", 2, [128, 2, T], BF16)
                GC = LRing(st, "GC", 2, [128, 2, T], BF16)
                ptr = Ring([sb(st, f"ptc{i}", [128, 512], BF16) for i in range(3)])
                recr = Ring([sb(st, f"recc{i}", [128, 512], F32) for i in range(2)])
                tmr = Ring([sb(st, f"tmc{i}", [128, 512], F32) for i in range(2)])
                fin = LRing(st, "finc", 3, [128, 512], BF16)
                accO = [ps(st, f"accOc{j}", [128, 512]) for j in range(2)]
                accL = ps(st, "accLc", [128, 512])
                spr = Ring([ps(st, f"spsc{i}", [128, 512]) for i in range(3)])
                for hc in range(4):
                    qcb, qsem = QC.next()
                    SP.dma(qcb[:], qcT_t[hc * 256:(hc + 1) * 256, :].rearrange("(j p) t -> p j t", p=128), qsem,
                           reads=[qcT], writes=[qcb])
                    gcb, gsem = GC.next()
                    SP.dma(gcb[:], gcT_t[hc * 256:(hc + 1) * 256, :].rearrange("(j p) t -> p j t", p=128), gsem,
                           reads=[gcT], writes=[gcb])
                    for qt in range(NQT):
                        tsl = slice(qt * 512, (qt + 1) * 512)
                        for mb in range(2):
                            sp_ = spr.next()
                            for j in range(2):
                                PE.op(lambda e, sp_=sp_, j=j, mb=mb, qcb=qcb, tsl=tsl, hc=hc: e.matmul(
                                    out=sp_[:], lhsT=kmT[:, hc * 2 + j, mb * 128:(mb + 1) * 128], rhs=qcb[:, j, tsl],
                                    start=(j == 0), stop=(j == 1)), reads=[kmT, qcb], writes=[sp_], inc=(j == 1))
                            pt = ptr.next()
                            ACT.op(lambda e, pt=pt, sp_=sp_: e.activation(out=pt[:], in_=sp_[:], func=AF.Exp),
                                   reads=[sp_], writes=[pt])
                            for jv in range(2):
                                PE.op(lambda e, jv=jv, mb=mb, pt=pt, hc=hc: e.matmul(
                                    out=accO[jv][:], lhsT=vm[:, mb, hc * 256 + jv * 128:hc * 256 + (jv + 1) * 128],
                                    rhs=pt[:], start=(mb == 0), stop=(mb == 1)),
                                    reads=[vm, pt], writes=[accO[jv]], inc=False)
                            PE.op(lambda e, mb=mb, pt=pt: e.matmul(out=accL[:], lhsT=ones[:], rhs=pt[:],
                                                                  start=(mb == 0), stop=(mb == 1)),
                                  reads=[ones, pt, vm], writes=[accL, accO[0], accO[1]])
                        rc = recr.next()
                        DVE.op(lambda e, rc=rc: e.reciprocal(out=rc[:], in_=accL[:]), reads=[accL], writes=[rc])
                        for jv in range(2):
                            tm = tmr.next()
                            DVE.op(lambda e, tm=tm, jv=jv, rc=rc: e.tensor_tensor(out=tm[:], in0=accO[jv][:], in1=rc[:],
                                                                                  op=ALU.mult),
                                   reads=[accO[jv], rc], writes=[tm])
                            fb_, fsem = fin.next()
                            DVE.op(lambda e, tm=tm, fb_=fb_, gcb=gcb, jv=jv, tsl=tsl: e.tensor_tensor(
                                out=fb_[:], in0=tm[:], in1=gcb[:, jv, tsl], op=ALU.mult),
                                reads=[tm, gcb], writes=[fb_])
                            SP.dma(ocT_t[hc * 256 + jv * 128:hc * 256 + (jv + 1) * 128, tsl], fb_[:], fsem,
                                   reads=[fb_], writes=[ocT])
                barrier()
            phase_end()
            if stop_after == "p4":
                break

            phase_begin()
            with contextlib.ExitStack() as st:
                OT = LRing(st, "OT", 1, [128, 3, 8, 512], BF16)
                WBR = LRing(st, "WBR", 6, [128, 8, 512], BF16)
                GL = LRing(st, "GL", 2, [128, 3, 4, 512], BF16)
                WO = LRing(st, "WO", 2, [128, DC, 512], BF16)
                XT = LRing(st, "XT", 2, [128, 4, 512], F32)
                XO = LRing(st, "XO", 2, [128, 4, 512], F32)
                yT = Ring([sb(st, f"yT{i}", [128, DC, 512], BF16) for i in range(1)])
                t3r = Ring([sb(st, f"t3{i}", [128, 512], F32) for i in range(6)])
                pbr = Ring([ps(st, f"pbr{i}", [128, 512]) for i in range(6)])
                por = Ring([ps(st, f"por{i}", [128, 512]) for i in range(2)])
                branches = (oaT_t, obT_t, ocT_t)
                branch_b = (oaT, obT, ocT)
                for tt in range(NQT):
                    tsl = slice(tt * 512, (tt + 1) * 512)
                    otb, otsem = OT.next()
                    for br in range(3):
                        SP.dma(otb[:, br, :, :], branches[br][:, tsl].rearrange("(c p) t -> p c t", p=128), otsem,
                               reads=[branch_b[br]], writes=[otb])
                    yb = yT.next()
                    for cb in range(4):
                        wbs = []
                        for br in range(3):
                            wbb, wsem = WBR.next()
                            POOL.dma(wbb[:], w_br_v[:, br, :, cb * 512:(cb + 1) * 512].rearrange("r p n -> p r n"),
                                     wsem, reads=[wf[l]], writes=[wbb])
                            wbs.append(wbb)
                        glb, glsem = GL.next()
                        for br in range(3):
                            SP.dma(glb[:, br, :, :],
                                   glT_t[br * 2048 + cb * 512:br * 2048 + (cb + 1) * 512, tsl].rearrange("(j p) t -> p j t", p=128),
                                   glsem, reads=[glT], writes=[glb])
                        for j in range(4):
                            ts3 = []
                            for br in range(3):
                                pb = pbr.next()
                                for c in range(8):
                                    PE.op(lambda e, pb=pb, br=br, c=c, j=j, wbs=wbs, otb=otb: e.matmul(
                                        out=pb[:], lhsT=wbs[br][:, c, j * 128:(j + 1) * 128], rhs=otb[:, br, c, :],
                                        start=(c == 0), stop=(c == 7)), reads=[wbs[br], otb], writes=[pb], inc=(c == 7))
                                t3 = t3r.next()
                                DVE.op(lambda e, t3=t3, pb=pb, glb=glb, br=br, j=j: e.tensor_tensor(
                                    out=t3[:], in0=pb[:], in1=glb[:, br, j, :], op=ALU.mult),
                                    reads=[pb, glb], writes=[t3])
                                ts3.append(t3)
                            POOL.op(lambda e, ts3=ts3: e.tensor_tensor(out=ts3[0][:], in0=ts3[0][:], in1=ts3[1][:], op=ALU.add),
                                    reads=[ts3[0], ts3[1]], writes=[ts3[0]])
                            POOL.op(lambda e, ts3=ts3, yb=yb, cb=cb, j=j: e.tensor_tensor(
                                out=yb[:, cb * 4 + j, :], in0=ts3[0][:], in1=ts3[2][:], op=ALU.add),
                                reads=[ts3[0], ts3[2]], writes=[yb])
                    for cb in range(4):
                        wob, wosem = WO.next()
                        for r in range(8):
                            POOL.dma(wob[:, 2 * r:2 * r + 2, :],
                                     w_out_v[r, :, cb * 512:(cb + 1) * 512].rearrange("(h p) n -> p h n", p=128),
                                     wosem, reads=[wf[l]], writes=[wob])
                        xtb, xtsem = XT.next()
                        SP.dma(xtb[:], xs_t[cb * 512:(cb + 1) * 512, tsl].rearrange("(j p) t -> p j t", p=128), xtsem,
                               reads=[xs], writes=[xtb])
                        xob, xosem = XO.next()
                        for j in range(4):
                            po = por.next()
                            for dc in range(DC):
                                PE.op(lambda e, po=po, wob=wob, dc=dc, j=j, yb=yb: e.matmul(
                                    out=po[:], lhsT=wob[:, dc, j * 128:(j + 1) * 128], rhs=yb[:, dc, :],
                                    start=(dc == 0), stop=(dc == DC - 1)), reads=[wob, yb], writes=[po], inc=(dc == DC - 1))
                            DVE.op(lambda e, po=po, xtb=xtb, xob=xob, j=j: e.tensor_tensor(
                                out=xob[:, j, :], in0=po[:], in1=xtb[:, j, :], op=ALU.add),
                                reads=[po, xtb], writes=[xob])
                        SP.dma(xs_t[cb * 512:(cb + 1) * 512, tsl].rearrange("(j p) t -> p j t", p=128), xob[:], xosem,
                               reads=[xob], writes=[xs])
                barrier()
            phase_end()
            if stop_after == "p5":
                break

        fin_sem = dsem("d_fin")
        if stop_after is None:
            with contextlib.ExitStack() as st:
                ob_ring = LRing(st, "onrm", 2, [128, DC, 256], F32)
                cur = {}

                def dst_fn(dc, t0, n):
                    if dc == 0:
                        cur["b"], cur["s"] = ob_ring.next()
                    return cur["b"][:, dc, :], cur["b"]

                def after(it, t0, n):
                    SP.dma(outT_v[:, :, t0:t0 + n], cur["b"][:], fin_sem, reads=[cur["b"]])

                rms_tiles(st, lambda t0, n: xs_v[:, :, t0:t0 + n], lambda dc: fing[:, dc:dc + 1],
                          T, 256, dst_fn, "f", [xs], after)
                barrier()
        else:
            for i in range(4):
                SP.dma(outT[i * 512:(i + 1) * 512, :], xs_t[i * 512:(i + 1) * 512, :], fin_sem, reads=[xs])
        fin_total = ctx.dma_issued[id(fin_sem)]

        def recfin(e):
            e.wait_ge(fin_sem, fin_total)
        ctx.prog["sp"].append(recfin)
        ctx.meta["sp"].append(([(fin_sem, fin_total)], []))
        build.last_ctx = ctx

        with nc.Block() as block:
            @block.tensor
            def _(e):
                for f in ctx.prog["pe"]:
                    f(e)

            @block.scalar
            def _(e):
                for f in ctx.prog["act"]:
                    f(e)

            @block.vector
            def _(e):
                for f in ctx.prog["dve"]:
                    f(e)

            @block.gpsimd
            def _(e):
                for f in ctx.prog["pool"]:
                    f(e)

            @block.sync
            def _(e):
                for f in ctx.prog["sp"]:
                    f(e)
    return nc


def _pm(v, ncol):
    v = np.asarray(v, np.float32).reshape(-1, ncol, 128)
    return np.ascontiguousarray(v.transpose(2, 0, 1).reshape(128, -1))


def make_in_maps(x, mem, norm_g, w_in, diff_lambda, diff_subln_g, hgrn_lb_raw, hgrn_norm_g,
                 mem_norm_g, w_kv_mem, w_branch, w_out, final_norm_g, n_layers=DEPTH, T=2048):
    f = np.float32
    x = np.asarray(x, f)
    mem = np.asarray(mem, f)
    w_in = np.asarray(w_in, f); w_kv_mem = np.asarray(w_kv_mem, f)
    w_branch = np.asarray(w_branch, f); w_out = np.asarray(w_out, f)
    common = dict(
        normg=_pm(norm_g, DC), memg=_pm(mem_norm_g, DC), fing=_pm(final_norm_g, DC),
        subg=_pm(diff_subln_g, 1), hgg=_pm(hgrn_norm_g, 1),
        lamb=np.ascontiguousarray(np.broadcast_to(np.asarray(diff_lambda, f).reshape(1, -1), (128, DEPTH * 256))),
        lbraw=_pm(np.asarray(hgrn_lb_raw, f).reshape(DEPTH * 8, 128), 1),
        ident=np.eye(128, dtype=f),
    )
    k = np.arange(128)[:, None]
    q = np.arange(512)[None, :]
    common["masks"] = np.ascontiguousarray(
        np.concatenate([((j * 128 + k) <= q).astype(f) for j in range(4)], axis=1))
    maps = []
    for c in range(8):
        b, hf = c // 2, c % 2
        m = dict(common)
        m["xT"] = np.ascontiguousarray(x[b, hf * T:(hf + 1) * T, :].T)
        m["memT"] = np.ascontiguousarray(mem[b].T)
        fl = np.zeros((128, 2), f)
        fl[:, 0] = 0.0 if hf == 1 else NEG
        fl[:, 1] = 1.0 if hf == 1 else 0.0
        m["flags"] = fl
        sh = np.empty((n_layers, WSH), f)
        for l in range(n_layers):
            o = 0
            sh[l, o:o + W_IN_SZ] = w_in[l, c * 256:(c + 1) * 256, :].reshape(-1); o += W_IN_SZ
            sh[l, o:o + W_KV_SZ] = w_kv_mem[l, c * 256:(c + 1) * 256, :].reshape(-1); o += W_KV_SZ
            sh[l, o:o + W_BR_SZ] = w_branch[l, :, c * 128:(c + 1) * 128, :].reshape(-1); o += W_BR_SZ
            sh[l, o:o + W_OUT_SZ] = w_out[l, c * 256:(c + 1) * 256, :].reshape(-1)
        m["wsh"] = sh.reshape(n_layers, 128, WCOLS)
        maps.append(m)
    return maps


_NC_CACHE = {}


def kernel(**inputs):
    if "nc" not in _NC_CACHE:
        _NC_CACHE["nc"] = build()
    nc = _NC_CACHE["nc"]
    maps = make_in_maps(**inputs)
    res = run_bass_kernel_spmd(nc, maps, core_ids=list(range(8)))
    out = np.empty((4, SEQ, D), np.float32)
    for c in range(8):
        b, hf = c // 2, c % 2
        out[b, hf * 2048:(hf + 1) * 2048, :] = res.results[c]["outT"].T
    return out
```
